# Optimizing a Trainium2 kernel written in Bass

```python
import jax, jax.numpy as jnp
from jax import lax
import numpy as np

D_MODEL = 4096
BATCH = 4
SEQ = 4096
DEPTH = 1

HEAD_DIM = 128
ATT_WIDTH = D_MODEL // 2
ATT_HEADS = ATT_WIDTH // HEAD_DIM
KV_HEADS = 4
GQA_GROUP = ATT_HEADS // KV_HEADS
IDX_HEADS = 16
IDX_DIM = 64
DSA_TOPK_MAX = 256
QUERY_BLOCK = 128
SSD_WIDTH = D_MODEL - ATT_WIDTH
SSD_HEAD_DIM = 64
SSD_HEADS = SSD_WIDTH // SSD_HEAD_DIM
SSD_GROUPS = 8
SSD_HEADS_PER_GROUP = SSD_HEADS // SSD_GROUPS
SSD_STATE = 128
SSD_CONV = 4
SSD_CHUNK = 128
XBC_WIDTH = SSD_WIDTH + 2 * SSD_GROUPS * SSD_STATE
MIX_WIDTH = ATT_WIDTH + SSD_WIDTH
N_EXPERT_GROUPS = 8
EXPERTS_PER_GROUP = 8
N_EXPERTS = N_EXPERT_GROUPS * EXPERTS_PER_GROUP
EXPERT_TOP_K = 2
D_EXPERT = 1024
MOE_BLOCK = 128
DEEPNORM_ALPHA = (2 * DEPTH) ** 0.25
DEEPNORM_BETA = (8 * DEPTH) ** -0.25
LN_EPS = 1e-5
RMS_EPS = 1e-5

PROJ_SIZES = (ATT_WIDTH, KV_HEADS * HEAD_DIM, KV_HEADS * HEAD_DIM, IDX_HEADS * IDX_DIM, IDX_DIM, IDX_HEADS,
              SSD_WIDTH, XBC_WIDTH, SSD_HEADS)
PROJ_WIDTH = sum(PROJ_SIZES)
PROJ_SPLITS = tuple(sum(PROJ_SIZES[:i + 1]) for i in range(len(PROJ_SIZES) - 1))

kernel_name = "hymba_dsa_ssd_hiermoe_deepnorm"


def _layer_norm(x, g, b):
    xf = x.astype(jnp.float32)
    mu = jnp.mean(xf, axis=-1, keepdims=True)
    var = jnp.mean(jnp.square(xf - mu), axis=-1, keepdims=True)
    y = (xf - mu) * lax.rsqrt(var + LN_EPS)
    return (y * g.astype(jnp.float32) + b.astype(jnp.float32)).astype(x.dtype)


def _dsa_attention(q, k, v, q_idx, k_idx, w_idx, kn_g, kn_b):
    bsz, seq, _ = q.shape
    top_k = min(DSA_TOPK_MAX, seq // 4)
    n_blk = seq // QUERY_BLOCK
    k = k.reshape(bsz, seq, KV_HEADS, HEAD_DIM)
    v = v.reshape(bsz, seq, KV_HEADS, HEAD_DIM)
    k_idx = _layer_norm(k_idx, kn_g, kn_b)
    w_idx = w_idx.astype(jnp.float32) * (IDX_HEADS ** -0.5 * IDX_DIM ** -0.5)

    def to_blocks(a, *tail):
        return jnp.moveaxis(a.reshape(bsz, n_blk, QUERY_BLOCK, *tail), 1, 0)

    q_b = to_blocks(q, KV_HEADS, GQA_GROUP, HEAD_DIM)
    qi_b = to_blocks(q_idx, IDX_HEADS, IDX_DIM)
    wi_b = to_blocks(w_idx, IDX_HEADS)
    key_pos = jnp.arange(seq)
    scale = HEAD_DIM ** -0.5

    def attend_block(args):
        blk, qb, qib, wib = args
        q_pos = blk * QUERY_BLOCK + jnp.arange(QUERY_BLOCK)
        dots = jnp.einsum('bthd,bsd->bths', qib, k_idx).astype(jnp.float32)
        score = jnp.einsum('bths,bth->bts', jax.nn.relu(dots), wib)
        score = jnp.where(key_pos[None, None, :] <= q_pos[None, :, None], score, -jnp.inf)
        sel_score, sel_idx = lax.top_k(score, top_k)
        valid = jnp.isfinite(sel_score)
        k_sel = jax.vmap(lambda kb, ib: kb[ib])(k, sel_idx)
        v_sel = jax.vmap(lambda vb, ib: vb[ib])(v, sel_idx)
        logits = jnp.einsum('btkgd,btnkd->btkgn', qb, k_sel).astype(jnp.float32) * scale
        logits = jnp.where(valid[:, :, None, None, :], logits, -jnp.inf)
        probs = jax.nn.softmax(logits, axis=-1).astype(v.dtype)
        out = jnp.einsum('btkgn,btnkd->btkgd', probs, v_sel)
        return out.reshape(bsz, QUERY_BLOCK, ATT_WIDTH)

    out = lax.map(attend_block, (jnp.arange(n_blk), q_b, qi_b, wi_b))
    return jnp.moveaxis(out, 0, 1).reshape(bsz, seq, ATT_WIDTH)


def _ssd_mixer(z, xbc, dt_raw, conv_w, conv_b, dt_bias, a_log, d_skip, norm_g):
    bsz, seq, _ = xbc.shape
    xbc = lax.conv_general_dilated(xbc, conv_w[:, None, :].astype(xbc.dtype), window_strides=(1,),
                                   padding=((SSD_CONV - 1, 0),), dimension_numbers=('NWC', 'WIO', 'NWC'),
                                   feature_group_count=XBC_WIDTH)
    xbc = jax.nn.silu(xbc + conv_b.astype(xbc.dtype)).astype(jnp.float32)
    xs, bm, cm = jnp.split(xbc, (SSD_WIDTH, SSD_WIDTH + SSD_GROUPS * SSD_STATE), axis=-1)
    dt = jax.nn.softplus(dt_raw.astype(jnp.float32) + dt_bias.astype(jnp.float32))
    a = -jnp.exp(a_log.astype(jnp.float32)).reshape(SSD_GROUPS, SSD_HEADS_PER_GROUP)
    nc, cq = seq // SSD_CHUNK, SSD_CHUNK
    xc = xs.reshape(bsz, nc, cq, SSD_GROUPS, SSD_HEADS_PER_GROUP, SSD_HEAD_DIM)
    dtc = dt.reshape(bsz, nc, cq, SSD_GROUPS, SSD_HEADS_PER_GROUP)
    bc = bm.reshape(bsz, nc, cq, SSD_GROUPS, SSD_STATE)
    cc = cm.reshape(bsz, nc, cq, SSD_GROUPS, SSD_STATE)
    acs = jnp.moveaxis(jnp.cumsum(dtc * a, axis=2), 2, -1)
    xdt = xc * dtc[..., None]
    causal = jnp.tril(jnp.ones((cq, cq), dtype=bool))
    seg = acs[..., :, None] - acs[..., None, :]
    lmat = jnp.exp(jnp.where(causal, seg, -jnp.inf))
    cb = jnp.einsum('bclgn,bcsgn->bcgls', cc, bc)
    y_diag = jnp.einsum('bcgls,bcgjls,bcsgjp->bclgjp', cb, lmat, xdt)
    decay_states = jnp.exp(acs[..., -1:] - acs)
    states = jnp.einsum('bclgn,bcgjl,bclgjp->bcgjpn', bc, decay_states, xdt)
    chunk_decay = jnp.exp(acs[..., -1])

    def step(h, inp):
        dec, st = inp
        return dec[..., None, None] * h + st, h

    h0 = jnp.zeros((bsz, SSD_GROUPS, SSD_HEADS_PER_GROUP, SSD_HEAD_DIM, SSD_STATE), jnp.float32)
    _, prev = lax.scan(step, h0, (jnp.moveaxis(chunk_decay, 1, 0), jnp.moveaxis(states, 1, 0)))
    prev = jnp.moveaxis(prev, 0, 1)
    y_off = jnp.einsum('bclgn,bcgjpn,bcgjl->bclgjp', cc, prev, jnp.exp(acs))
    d = d_skip.astype(jnp.float32).reshape(SSD_GROUPS, SSD_HEADS_PER_GROUP)[..., None]
    y = (y_diag + y_off + xc * d).reshape(bsz, seq, SSD_WIDTH)
    gated = (y * jax.nn.silu(z.astype(jnp.float32))).reshape(bsz, seq, SSD_GROUPS, SSD_WIDTH // SSD_GROUPS)
    gated = gated * lax.rsqrt(jnp.mean(jnp.square(gated), axis=-1, keepdims=True) + RMS_EPS)
    return (gated.reshape(bsz, seq, SSD_WIDTH) * norm_g.astype(jnp.float32)).astype(z.dtype)


def _hier_moe(h, w_rg, b_rg, w_re, b_re, w_gate, w_up, w_down):
    bsz, seq, dm = h.shape
    n_tok = bsz * seq
    hf = h.reshape(n_tok, dm)
    g_logits = jnp.dot(hf, w_rg).astype(jnp.float32) + b_rg.astype(jnp.float32)
    g_prob = jax.nn.softmax(g_logits, axis=-1)
    g_w, g_sel = lax.top_k(g_prob, 1)
    e_logits = (jnp.dot(hf, w_re).astype(jnp.float32) + b_re.astype(jnp.float32))
    e_logits = e_logits.reshape(n_tok, N_EXPERT_GROUPS, EXPERTS_PER_GROUP)
    e_logits = jnp.take_along_axis(e_logits, g_sel[:, :, None], axis=1)[:, 0]
    e_val, e_loc = lax.top_k(e_logits, EXPERT_TOP_K)
    gate = g_w * jax.nn.softmax(e_val, axis=-1)
    expert_id = g_sel * EXPERTS_PER_GROUP + e_loc

    m = n_tok * EXPERT_TOP_K
    e_flat = expert_id.reshape(m).astype(jnp.int32)
    tok_flat = jnp.repeat(jnp.arange(n_tok, dtype=jnp.int32), EXPERT_TOP_K)
    g_flat = gate.reshape(m)
    order = jnp.argsort(e_flat)
    se, stok, sg = e_flat[order], tok_flat[order], g_flat[order]
    counts = jnp.bincount(e_flat, length=N_EXPERTS)
    starts = jnp.cumsum(counts) - counts
    pcounts = (counts + MOE_BLOCK - 1) // MOE_BLOCK * MOE_BLOCK
    pends = jnp.cumsum(pcounts)
    pstarts = pends - pcounts
    dest = pstarts[se] + jnp.arange(m, dtype=jnp.int32) - starts[se]
    n_blocks = -(-m // MOE_BLOCK) + N_EXPERTS
    n_rows = n_blocks * MOE_BLOCK
    row_tok = jnp.zeros((n_rows,), jnp.int32).at[dest].set(stok)
    row_gate = jnp.zeros((n_rows,), jnp.float32).at[dest].set(sg)
    block_e = jnp.minimum(jnp.searchsorted(pends, jnp.arange(n_blocks) * MOE_BLOCK, side='right'),
                          N_EXPERTS - 1).astype(jnp.int32)

    def expert_block(args):
        toks, e = args
        xb = hf[toks]
        act = jax.nn.silu(jnp.dot(xb, w_gate[e])) * jnp.dot(xb, w_up[e])
        return jnp.dot(act, w_down[e])

    out = lax.map(expert_block, (row_tok.reshape(n_blocks, MOE_BLOCK), block_e)).reshape(n_rows, dm)
    y = jax.ops.segment_sum(out * row_gate[:, None].astype(out.dtype), row_tok, num_segments=n_tok)
    return y.reshape(bsz, seq, dm).astype(h.dtype)


def setup_inputs(seed: int = 0) -> dict:
    key = jax.random.key(seed)
    ks = jax.random.split(key, 24)
    f32 = jnp.float32
    nrm = lambda k, shape, s: jax.random.normal(k, shape, f32) * s
    x = jax.random.normal(ks[0], (BATCH, SEQ, D_MODEL), f32)
    col_scale = jnp.concatenate([
        jnp.ones((ATT_WIDTH + KV_HEADS * HEAD_DIM,), f32),
        jnp.full((KV_HEADS * HEAD_DIM,), DEEPNORM_BETA, f32),
        jnp.ones((IDX_HEADS * IDX_DIM + IDX_DIM + IDX_HEADS + SSD_WIDTH,), f32),
        jnp.full((SSD_WIDTH,), DEEPNORM_BETA, f32),
        jnp.ones((2 * SSD_GROUPS * SSD_STATE + SSD_HEADS,), f32)])
    w_in = nrm(ks[1], (DEPTH, D_MODEL, PROJ_WIDTH), D_MODEL ** -0.5) * col_scale
    idx_kn_g = 1.0 + nrm(ks[2], (DEPTH, IDX_DIM), 0.02)
    idx_kn_b = nrm(ks[3], (DEPTH, IDX_DIM), 0.02)
    conv_w = nrm(ks[4], (DEPTH, SSD_CONV, XBC_WIDTH), SSD_CONV ** -0.5)
    conv_b = nrm(ks[5], (DEPTH, XBC_WIDTH), 0.02)
    u = jax.random.uniform(ks[6], (DEPTH, SSD_HEADS), f32)
    dt0 = jnp.exp(u * (jnp.log(0.1) - jnp.log(0.001)) + jnp.log(0.001))
    dt_bias = dt0 + jnp.log(-jnp.expm1(-dt0))
    a_log = jnp.log(jax.random.uniform(ks[7], (DEPTH, SSD_HEADS), f32, 1.0, 16.0))
    d_skip = 1.0 + nrm(ks[8], (DEPTH, SSD_HEADS), 0.02)
    ssd_norm_g = 1.0 + nrm(ks[9], (DEPTH, SSD_WIDTH), 0.02)
    w_out = nrm(ks[10], (DEPTH, MIX_WIDTH, D_MODEL), MIX_WIDTH ** -0.5 * DEEPNORM_BETA)
    ln1_g = 1.0 + nrm(ks[11], (DEPTH, D_MODEL), 0.02)
    ln1_b = nrm(ks[12], (DEPTH, D_MODEL), 0.02)
    w_rg = nrm(ks[13], (DEPTH, D_MODEL, N_EXPERT_GROUPS), D_MODEL ** -0.5)
    b_rg = nrm(ks[14], (DEPTH, N_EXPERT_GROUPS), 0.01)
    w_re = nrm(ks[15], (DEPTH, D_MODEL, N_EXPERTS), D_MODEL ** -0.5)
    b_re = nrm(ks[16], (DEPTH, N_EXPERTS), 0.01)
    w_gate = nrm(ks[17], (DEPTH, N_EXPERTS, D_MODEL, D_EXPERT), D_MODEL ** -0.5 * DEEPNORM_BETA)
    w_up = nrm(ks[18], (DEPTH, N_EXPERTS, D_MODEL, D_EXPERT), D_MODEL ** -0.5 * DEEPNORM_BETA)
    w_down = nrm(ks[19], (DEPTH, N_EXPERTS, D_EXPERT, D_MODEL), D_EXPERT ** -0.5 * DEEPNORM_BETA)
    ln2_g = 1.0 + nrm(ks[20], (DEPTH, D_MODEL), 0.02)
    ln2_b = nrm(ks[21], (DEPTH, D_MODEL), 0.02)
    return {"x": x, "w_in": w_in, "idx_kn_g": idx_kn_g, "idx_kn_b": idx_kn_b, "conv_w": conv_w,
            "conv_b": conv_b, "dt_bias": dt_bias, "a_log": a_log, "d_skip": d_skip,
            "ssd_norm_g": ssd_norm_g, "w_out": w_out, "ln1_g": ln1_g, "ln1_b": ln1_b,
            "w_rg": w_rg, "b_rg": b_rg, "w_re": w_re, "b_re": b_re, "w_gate": w_gate,
            "w_up": w_up, "w_down": w_down, "ln2_g": ln2_g, "ln2_b": ln2_b}


def reference(x, w_in, idx_kn_g, idx_kn_b, conv_w, conv_b, dt_bias, a_log, d_skip, ssd_norm_g, w_out,
              ln1_g, ln1_b, w_rg, b_rg, w_re, b_re, w_gate, w_up, w_down, ln2_g, ln2_b):
    for l in range(DEPTH):
        proj = jnp.einsum('bsd,dp->bsp', x, w_in[l])
        q, k, v, q_idx, k_idx, w_idx, z, xbc, dt_raw = jnp.split(proj, PROJ_SPLITS, axis=-1)
        att = _dsa_attention(q, k, v, q_idx, k_idx, w_idx, idx_kn_g[l], idx_kn_b[l])
        ssd = _ssd_mixer(z, xbc, dt_raw, conv_w[l], conv_b[l], dt_bias[l], a_log[l], d_skip[l], ssd_norm_g[l])
        mixed = jnp.einsum('bsm,md->bsd', jnp.concatenate([att, ssd.astype(att.dtype)], axis=-1), w_out[l])
        x = _layer_norm(DEEPNORM_ALPHA * x + mixed, ln1_g[l], ln1_b[l])
        ffn = _hier_moe(x, w_rg[l], b_rg[l], w_re[l], b_re[l], w_gate[l], w_up[l], w_down[l])
        x = _layer_norm(DEEPNORM_ALPHA * x + ffn, ln2_g[l], ln2_b[l])
    return x
```

```python
import numpy as np
import ml_dtypes
from contextlib import ExitStack
import concourse.bass as bass
import concourse.mybir as mybir
from concourse.bass_utils import run_bass_kernel_spmd

F32 = mybir.dt.float32
BF16 = mybir.dt.bfloat16
I32 = mybir.dt.int32
AF = mybir.ActivationFunctionType
ALU = mybir.AluOpType
AX = mybir.AxisListType

COMPUTE = ('pe', 'act', 'dve', 'pool')
QUEUES = ('sp', 'act', 'pool')
ALLENG = ('pe', 'act', 'dve', 'pool', 'sp')
CHUNK = 4096
DK = 12


class _Rec:
    def __getattr__(self, name):
        def f(*a, **k):
            self.call = (name, a, k)
            return self
        return f


class Sched:
    def __init__(self, nc, es):
        self.nc = nc
        self.es = es
        self.streams = {e: [] for e in ALLENG}
        self.seq = {e: 0 for e in COMPUTE}
        self.sems = {e: [] for e in COMPUTE}
        self.dcount = {'d' + q: 0 for q in QUEUES}
        self.dsems = {'d' + q: [es.enter_context(nc.semaphore(f"d_{q}_{i}")) for i in range(DK)] for q in QUEUES}
        self.lastw = {}
        self.readers = {}
        self.known = {e: {} for e in ALLENG}
        self.pending_noinc = {e: False for e in COMPUTE}
        self.nops = 0

    def _sem(self, eng, seq):
        c = (seq - 1) // CHUNK
        while len(self.sems[eng]) <= c:
            self.sems[eng].append(self.es.enter_context(self.nc.semaphore(f"s_{eng}_{len(self.sems[eng])}")))
        return self.sems[eng][c], (seq - 1) % CHUNK + 1

    def _wait(self, eng, dep):
        src, n = dep
        if src == eng and eng == 'pe':
            return
        kk = src if src in COMPUTE else (src, n % DK)
        if self.known[eng].get(kk, 0) >= n:
            return
        self.known[eng][kk] = n
        if src in COMPUTE:
            sem, val = self._sem(src, n)
        else:
            i = n - 1
            sem = self.dsems[src][i % DK]
            val = 16 * (i // DK + 1)
        self.streams[eng].append(lambda e, sem=sem, val=val: e.wait_ge(sem, val))

    def _deps(self, r, w):
        deps = []
        for k in r:
            if k in self.lastw:
                deps.append(self.lastw[k])
        for k in w:
            if k in self.lastw:
                deps.append(self.lastw[k])
            for d in self.readers.get(k, {}).values():
                deps.append(d)
        return deps

    def _record(self, ident, r, w):
        src, n = ident
        kk = src if src in COMPUTE else (src, n % DK)
        for k in r:
            d = self.readers.setdefault(k, {})
            if kk not in d or d[kk][1] < n:
                d[kk] = (src, n)
        for k in w:
            self.lastw[k] = ident
            self.readers[k] = {}

    def op(self, eng, fn, r=(), w=(), inc=True):
        assert eng in COMPUTE
        self.nops += 1
        rec = _Rec()
        fn(rec)
        name_, a_, k_ = rec.call
        fn = lambda e, name_=name_, a_=a_, k_=k_: getattr(e, name_)(*a_, **k_)
        for d in self._deps(r, w):
            self._wait(eng, d)
        n = self.seq[eng] + 1
        if inc:
            self.seq[eng] = n
            sem, _ = self._sem(eng, n)
            self.streams[eng].append(lambda e, fn=fn, sem=sem: fn(e).then_inc(sem, 1))
            self.pending_noinc[eng] = False
        else:
            self.streams[eng].append(lambda e, fn=fn: fn(e))
            self.pending_noinc[eng] = True
        self._record((eng, n), r, w)

    def dma(self, q, out, in_, r=(), w=(), indirect=None, **kw):
        self.nops += 1
        for d in self._deps(r, w):
            self._wait(q, d)
        dq = 'd' + q
        i = self.dcount[dq]
        self.dcount[dq] = i + 1
        if i >= DK:
            self._wait(q, (dq, i - DK + 1))
        sem = self.dsems[dq][i % DK]
        if indirect is None:
            self.streams[q].append(lambda e, out=out, in_=in_, sem=sem, kw=kw:
                                   e.dma_start(out=out, in_=in_, **kw).then_inc(sem, 16))
        else:
            self.streams[q].append(lambda e, out=out, in_=in_, sem=sem, kw=indirect:
                                   e.indirect_dma_start(out=out, in_=in_, **kw).then_inc(sem, 16))
        self._record((dq, i + 1), r, w)

    def barrier(self):
        for e in COMPUTE:
            assert not self.pending_noinc[e], e
        for tgt in ALLENG:
            for q in QUEUES:
                dq = 'd' + q
                n = self.dcount[dq]
                for i in range(max(0, n - DK), n):
                    self._wait(tgt, (dq, i + 1))
            for e in COMPUTE:
                if self.seq[e] > 0:
                    self._wait(tgt, (e, self.seq[e]))

    def emit(self, block):
        m = {'pe': block.tensor, 'act': block.scalar, 'dve': block.vector, 'pool': block.gpsimd, 'sp': block.sync}
        for eng, deco in m.items():
            stream = self.streams[eng]

            def body(e, stream=stream):
                for f in stream:
                    f(e)
            deco(body)


class Cfg:
    def __init__(self, D=4096, SEQ=4096, B=4, KV=4, SG=8, DE=1024, TT=1024, ETOK=512):
        self.D, self.SEQ, self.B, self.KV, self.SG, self.DE, self.TT, self.ETOK = D, SEQ, B, KV, SG, DE, TT, ETOK
        self.T = SEQ // 2
        self.W = SEQ
        self.KC = D // 128
        self.AW = D // 2
        self.AH = self.AW // 128
        self.G = self.AH // KV
        self.IH, self.ID = 16, 64
        self.SW = D - self.AW
        self.SH = self.SW // 64
        self.HPG = self.SH // SG
        assert self.HPG == 4
        self.NS = 128
        self.XBC = self.SW + 2 * SG * 128
        self.E, self.EG, self.EPG = 64, 8, 8
        self.CAP = 128
        self.TOPK = min(256, SEQ // 4)
        self.NTO = self.T // 128
        self.NTW = self.W // 128
        self.sizes = (self.AW, KV * 128, KV * 128, self.IH * self.ID, self.ID, self.IH, self.SW, self.XBC, self.SH)
        self.PROJ = sum(self.sizes)
        o = np.cumsum((0,) + self.sizes)
        (self.o_q, self.o_k, self.o_v, self.o_qi, self.o_ki, self.o_wi, self.o_z, self.o_xbc, self.o_dt) = [int(v) for v in o[:9]]
        self.alpha = 2.0 ** 0.25
        self.n_cores = 2 * B


FULL = Cfg(ETOK=256)


class Builder:
    def __init__(self, cfg, debug=()):
        self.cfg = cfg
        self.debug = set(debug)
        self.nc = bass.Bass("TRN2", target_bir_lowering=False)
        self.ins = {}
        self.scr = {}

    def din(self, name, shape, dt=F32):
        t = self.nc.dram_tensor(name, list(shape), dt, kind="ExternalInput").ap()
        self.ins[name] = (tuple(shape), dt)
        return t

    def dscr(self, name, shape, dt):
        kind = "ExternalOutput" if name in self.debug else "Internal"
        t = self.nc.dram_tensor(name, list(shape), dt, kind=kind).ap()
        self.scr[name] = t
        return t

    def reset(self):
        self.S.barrier()
        self.off = self.res_off

    def alloc(self, nfree, dt=F32, parts=128):
        n32 = (nfree * (4 if dt in (F32, I32) else 2) + 3) // 4
        n32 = (n32 + 1) // 2 * 2
        assert self.off + n32 <= self.ARENA, (self.off, n32, self.ARENA)
        ap = self.arena[0:parts, self.off:self.off + n32]
        self.off += n32
        if dt != F32:
            ap = ap.bitcast(dt)
        return ap[:, 0:nfree]

    def build(self):
        cfg = self.cfg
        nc = self.nc
        D, W, T, KC = cfg.D, cfg.W, cfg.T, cfg.KC
        d = self.din
        self.xT = d("xT_win", [D, W])
        self.x_own = d("x_own", [T, D])
        self.kpos = d("kpos", [128, W])
        self.flag = d("flag", [128, 1])
        self.w_in = d("w_in", [D, cfg.PROJ])
        self.w_small = d("w_small", [D, 128])
        self.w_out = d("w_out", [D, D])
        self.w_r = d("w_r", [D, 72])
        self.w_gate = d("w_gate", [cfg.E, D, cfg.DE])
        self.w_up = d("w_up", [cfg.E, D, cfg.DE])
        self.w_down = d("w_down", [cfg.E, cfg.DE, D])
        self.c_knb = d("c_knb", [128, 128])
        self.c_dtb = d("c_dtb", [128, 2 * cfg.SH])
        self.c_dvec = d("c_dvec", [128, cfg.SW])
        self.c_normg = d("c_normg", [128, cfg.SW])
        self.c_conv = d("c_conv", [128, (cfg.XBC // 128) * 5])
        self.c_ln = d("c_ln", [4, D])
        self.c_rb = d("c_rb", [128, 72])
        self.c_f32 = d("c_f32", [128, 128 * 3 + cfg.NTO + 64 + 128])
        self.c_bf = d("c_bf", [128, 128 * 3 + 512 + 2048], BF16)
        self.out = nc.dram_tensor("out", [T, D], F32, kind="ExternalOutput").ap()
        s = self.dscr
        self.QT = s("QT", [cfg.AW, T], BF16)
        self.KT = s("KT", [cfg.KV * 128, W], BF16)
        self.V = s("V", [W, cfg.KV * 128], BF16)
        self.QiT = s("QiT", [cfg.IH * cfg.ID, T], BF16)
        self.KiT = s("KiT", [cfg.ID, W], BF16)
        self.Z = s("Z", [T, cfg.SW], F32)
        self.XT = s("XT", [cfg.XBC, W], F32)
        self.XC = s("XC", [cfg.XBC, W], BF16)
        self.MixT = s("MixT", [D, T], BF16)
        self.X1 = s("X1", [T, D], F32)
        self.X1b = s("X1b", [T, D], BF16)
        self.XE = s("XE", [cfg.E * cfg.CAP, D], BF16)
        self.YE = s("YE", [cfg.E * cfg.CAP, D], F32)
        if 'dbg' in self.debug:
            self.DBG = nc.dram_tensor("DBG", [T, 128], F32, kind="ExternalOutput").ap()

        with ExitStack() as es:
            self.es = es
            self.ARENA = 52 * 1024 - 512
            self.arena = es.enter_context(nc.sbuf_tensor("arena", [128, self.ARENA], F32))
            self.banks = [es.enter_context(nc.psum_tensor(f"bank{i}", [128, 512], F32)) for i in range(8)]
            self.S = Sched(nc, es)
            self.off = 0
            self.load_consts()
            self.res_off = self.off
            import os
            ph = os.environ.get("PHASES", "ABCDEF")
            for p_ in ph:
                getattr(self, "phase_" + p_)()
            self.S.barrier()
            with nc.Block() as block:
                self.S.emit(block)
        return nc

    def load_consts(self):
        cfg, S = self.cfg, self.S
        NTO = cfg.NTO
        cf = self.alloc(128 * 3 + NTO + 64 + 128)
        S.dma('sp', cf, self.c_f32, w=['cf'])
        self.ident_f = cf[:, 0:128]
        self.triU_f = cf[:, 128:256]
        self.ones_f = cf[:, 256:384]
        self.qpos = cf[:, 384:384 + NTO]
        self.iota_e = cf[:, 384 + NTO:384 + NTO + 64]
        self.mask8 = cf[:, 384 + NTO + 64:384 + NTO + 64 + 128]
        cb = self.alloc(128 * 3 + 512 + 2048, BF16)
        S.dma('sp', cb, self.c_bf, w=['cb'])
        self.ident_b = cb[:, 0:128]
        self.ones_b = cb[:, 128:256]
        self.stri_b = cb[:, 256:384]
        self.negmask4 = cb[:, 384:896]
        self.CM = cb[:, 896:896 + 2048]
        self.flag_sb = self.alloc(2)
        S.dma('sp', self.flag_sb[:, 0:1], self.flag, w=['flag'])
        cc_ = self.alloc(4)
        self.eps5 = cc_[:, 0:1]
        self.one1 = cc_[:, 1:2]
        S.op('dve', lambda e: e.memset(cc_[:, 0:1], 1e-5), w=['eps5'])
        S.op('dve', lambda e: e.memset(cc_[:, 1:2], 1.0), w=['one1'])
        self.slots = self.alloc(NTO * 2, I32)
        self.gates = self.alloc(NTO * 2)
        self.cnt_b = self.alloc(64)
        self.Wi_sb = self.alloc(NTO * 16)
        self.dt_sb = self.alloc(cfg.NTW * cfg.SH)

    def phase_A(self):
        cfg, S, nc = self.cfg, self.S, self.nc
        D, W, T, KC, TT = cfg.D, cfg.W, cfg.T, cfg.KC, cfg.TT
        self.reset()
        xT = self.alloc(KC * TT, BF16).rearrange("p (k t) -> p k t", t=TT)
        NWB = 2
        wbs = [self.alloc(KC * 512, BF16).rearrange("p (k c) -> p k c", c=512) for _ in range(NWB)]
        NST = 4
        st32 = [self.alloc(512) for _ in range(NST)]
        knb = self.alloc(128)
        dtb = self.alloc(2 * cfg.SH)
        S.dma('sp', knb, self.c_knb, w=['knb'])
        S.dma('sp', dtb, self.c_dtb, w=['dtb'])
        sm = [self.alloc(128) for _ in range(2)]
        smc = [self.alloc(64) for _ in range(2)]
        smb = [self.alloc(64, BF16) for _ in range(2)]
        kst = [self.alloc(128, BF16, parts=64) for _ in range(2)]
        stat = [self.alloc(8) for _ in range(2)]
        dtt = [self.alloc(4 * cfg.SH) for _ in range(2)]
        xTv = self.xT.rearrange("(k p) w -> p k w", p=128)
        w_in_v = self.w_in.rearrange("(k p) c -> p k c", p=128)
        w_sm_v = self.w_small.rearrange("(k p) c -> p k c", p=128)
        Wi3 = self.Wi_sb.rearrange("p (n h) -> p n h", h=16)
        dt3 = self.dt_sb.rearrange("p (n h) -> p n h", h=cfg.SH)
        cnt = {'w': 0, 'st': 0, 'bank': 0, 'sm': 0}

        def next_bank():
            b = cnt['bank'] % 6
            cnt['bank'] += 1
            return b

        def evac(ps_ap, st_ap, bank_key, st_key):
            i = cnt['st']
            eng = 'act' if i % 2 == 0 else 'dve'
            if eng == 'act':
                S.op('act', lambda e: e.copy(out=st_ap, in_=ps_ap), r=[bank_key], w=[st_key])
            else:
                S.op('dve', lambda e: e.tensor_copy(out=st_ap, in_=ps_ap), r=[bank_key], w=[st_key])

        NTT = W // TT
        for tt in range(NTT):
            own = tt >= NTT // 2
            tok0 = tt * TT
            S.dma('pool', xT, xTv[:, :, tok0:tok0 + TT], w=['xT'])
            jobs = []
            if own:
                for c in range(cfg.AW // 512):
                    jobs.append(('FM', cfg.o_q + c * 512, 512, self.QT, c * 512, BF16, True))
            for c in range(max(1, cfg.KV * 128 // 512)):
                n = min(512, cfg.KV * 128)
                jobs.append(('FM', cfg.o_k + c * 512, n, self.KT, c * 512, BF16, False))
            if own:
                for c in range(cfg.IH * cfg.ID // 512):
                    jobs.append(('FM', cfg.o_qi + c * 512, 512, self.QiT, c * 512, BF16, True))
            for c in range(cfg.XBC // 512):
                jobs.append(('FM', cfg.o_xbc + c * 512, 512, self.XT, c * 512, F32, False))
            for c in range(max(1, cfg.KV * 128 // 512)):
                n = min(512, cfg.KV * 128)
                jobs.append(('TM', cfg.o_v + c * 512, n, self.V, c * 512, BF16, False))
            if own:
                for c in range(cfg.SW // 512):
                    jobs.append(('TM', cfg.o_z + c * 512, 512, self.Z, c * 512, F32, True))
            jobs.append(('SMALL', 0, 128, None, 0, F32, False))
            for (kind, c0, ncols, dest, r0, ddt, ownrel) in jobs:
                wi = cnt['w'] % NWB
                cnt['w'] += 1
                wb = wbs[wi]
                wkey = f'wb{wi}'
                if kind == 'SMALL':
                    S.dma('pool', wb[:, :, 0:128], w_sm_v, w=[wkey])
                else:
                    S.dma('pool', wb[:, :, 0:ncols], w_in_v[:, :, c0:c0 + ncols], w=[wkey])
                tcol0 = tok0 - (T if ownrel else 0)
                if kind == 'FM':
                    for mg in range(ncols // 128):
                        for th in range(TT // 512):
                            b = next_bank()
                            ps = self.banks[b]
                            bk = f'bank{b}'
                            for kc in range(KC):
                                S.op('pe', lambda e, ps=ps, wb=wb, kc=kc, mg=mg, th=th: e.matmul(
                                    ps[:, :], wb[:, kc, mg * 128:(mg + 1) * 128], xT[:, kc, th * 512:(th + 1) * 512],
                                    start=(kc == 0), stop=(kc == KC - 1)),
                                    r=[wkey, 'xT'], w=[bk], inc=(kc == KC - 1))
                            si = cnt['st'] % NST
                            stt = st32[si] if ddt == F32 else st32[si].bitcast(BF16)[:, 0:512]
                            evac(ps[:, :], stt, bk, f'st{si}')
                            cnt['st'] += 1
                            S.dma('sp', dest[r0 + mg * 128:r0 + (mg + 1) * 128, tcol0 + th * 512:tcol0 + (th + 1) * 512],
                                  stt, r=[f'st{si}'])
                elif kind == 'TM':
                    for ts in range(TT // 128):
                        b = next_bank()
                        ps = self.banks[b]
                        bk = f'bank{b}'
                        for kc in range(KC):
                            S.op('pe', lambda e, ps=ps, wb=wb, kc=kc, ts=ts, ncols=ncols: e.matmul(
                                ps[:, 0:ncols], xT[:, kc, ts * 128:(ts + 1) * 128], wb[:, kc, 0:ncols],
                                start=(kc == 0), stop=(kc == KC - 1)),
                                r=[wkey, 'xT'], w=[bk], inc=(kc == KC - 1))
                        si = cnt['st'] % NST
                        stt = (st32[si] if ddt == F32 else st32[si].bitcast(BF16)[:, 0:512])[:, 0:ncols]
                        evac(ps[:, 0:ncols], stt, bk, f'st{si}')
                        cnt['st'] += 1
                        S.dma('sp', dest[tcol0 + ts * 128:tcol0 + (ts + 1) * 128, r0:r0 + ncols],
                              stt, r=[f'st{si}'])
                else:
                    for ts in range(TT // 128):
                        wt = tt * (TT // 128) + ts
                        b = next_bank()
                        ps = self.banks[b]
                        bk = f'bank{b}'
                        for kc in range(KC):
                            S.op('pe', lambda e, ps=ps, wb=wb, kc=kc, ts=ts: e.matmul(
                                ps[:, 0:128], xT[:, kc, ts * 128:(ts + 1) * 128], wb[:, kc, 0:128],
                                start=(kc == 0), stop=(kc == KC - 1)),
                                r=[wkey, 'xT'], w=[bk], inc=(kc == KC - 1))
                        j = cnt['sm'] % 2
                        cnt['sm'] += 1
                        smj, smcj, smbj, kstj, stj, dttj = sm[j], smc[j], smb[j], kst[j], stat[j], dtt[j]
                        k = f'sm{j}'
                        S.op('act', lambda e, ps=ps, smj=smj: e.copy(out=smj, in_=ps[:, 0:128]), r=[bk], w=[k])
                        S.op('dve', lambda e, smj=smj, stj=stj: e.tensor_reduce(out=stj[:, 0:1], in_=smj[:, 0:64], axis=AX.X, op=ALU.add),
                             r=[k], w=[k + 'a'])
                        S.op('dve', lambda e, stj=stj: e.tensor_scalar(out=stj[:, 1:2], in0=stj[:, 0:1], scalar1=-1.0 / 64, scalar2=None, op0=ALU.mult),
                             r=[k + 'a'], w=[k + 'b'])
                        S.op('dve', lambda e, smj=smj, smcj=smcj, stj=stj: e.tensor_scalar(out=smcj, in0=smj[:, 0:64], scalar1=stj[:, 1:2], scalar2=None, op0=ALU.add),
                             r=[k, k + 'b'], w=[k + 'c'])
                        S.op('act', lambda e, smcj=smcj, smbj=smbj, stj=stj: e.activation(out=smbj, in_=smcj, func=AF.Square, accum_out=stj[:, 2:3]),
                             r=[k + 'c'], w=[k + 'd', k + 'junk'])
                        S.op('act', lambda e, stj=stj: e.activation(out=stj[:, 3:4], in_=stj[:, 2:3], func=AF.Sqrt, scale=1.0 / 64, bias=self.eps5[:, 0:1]),
                             r=[k + 'd'], w=[k + 'e'])
                        S.op('dve', lambda e, stj=stj: e.reciprocal(out=stj[:, 4:5], in_=stj[:, 3:4]), r=[k + 'e'], w=[k + 'f'])
                        S.op('dve', lambda e, smcj=smcj, stj=stj: e.scalar_tensor_tensor(out=smcj, in0=smcj, scalar=stj[:, 4:5], in1=knb[:, 0:64], op0=ALU.mult, op1=ALU.mult),
                             r=[k + 'c', k + 'f', 'knb'], w=[k + 'c'])
                        S.op('dve', lambda e, smcj=smcj, smbj=smbj: e.tensor_tensor(out=smbj, in0=smcj, in1=knb[:, 64:128], op=ALU.add),
                             r=[k + 'c', 'knb', k + 'junk'], w=[k + 'g'])
                        b2 = next_bank()
                        ps2 = self.banks[b2][0:64, 0:64].bitcast(BF16)
                        S.op('pe', lambda e, ps2=ps2, smbj=smbj: e.transpose(out=ps2, in_=smbj, identity=self.ident_b),
                             r=[k + 'g', 'cb'], w=[f'bank{b2}'])
                        S.op('act', lambda e, ps2=ps2, kstj=kstj: e.copy(out=kstj, in_=ps2), r=[f'bank{b2}'], w=[k + 'h'])
                        S.dma('sp', self.KiT[:, wt * 128:(wt + 1) * 128], kstj, r=[k + 'h'])
                        if own:
                            ot = wt - cfg.NTO
                            S.op('act', lambda e, smj=smj, ot=ot: e.mul(out=Wi3[:, ot, :], in_=smj[:, 64:80], mul=1.0 / 32),
                                 r=[k], w=['Wi'])
                        SH = cfg.SH
                        xa, xb, xc_, xd = dttj[:, 0:SH], dttj[:, SH:2 * SH], dttj[:, 2 * SH:3 * SH], dttj[:, 3 * SH:4 * SH]
                        S.op('dve', lambda e, smj=smj, xa=xa: e.tensor_tensor(out=xa, in0=smj[:, 80:80 + SH], in1=dtb[:, 0:SH], op=ALU.add),
                             r=[k, 'dtb'], w=[k + 'x'])
                        S.op('act', lambda e, xa=xa, xb=xb: e.activation(out=xb, in_=xa, func=AF.Abs),
                             r=[k + 'x'], w=[k + 'y'])
                        S.op('act', lambda e, xb=xb, xc_=xc_: e.activation(out=xc_, in_=xb, func=AF.Exp, scale=-1.0), r=[k + 'y'], w=[k + 'z'])
                        S.op('act', lambda e, xc_=xc_, xd=xd: e.activation(out=xd, in_=xc_, func=AF.Ln, bias=self.one1[:, 0:1]), r=[k + 'z'], w=[k + 'u'])
                        S.op('dve', lambda e, xa=xa, xd=xd, wt=wt: e.scalar_tensor_tensor(out=dt3[:, wt, :], in0=xa, scalar=0.0, in1=xd, op0=ALU.max, op1=ALU.add),
                             r=[k + 'x', k + 'u'], w=['dt'])

    def phase_B(self):
        cfg, S = self.cfg, self.S
        W = cfg.W
        NCC = cfg.XBC // 128
        self.reset()
        cw = self.alloc(NCC * 5)
        S.dma('sp', cw, self.c_conv, w=['cw'])
        X = [self.alloc(3 + W) for _ in range(2)]
        acc = [self.alloc(W) for _ in range(2)]
        ob = [self.alloc(W, BF16) for _ in range(2)]
        for j in range(2):
            S.op('dve', lambda e, j=j: e.memset(X[j][:, 0:3], 0.0), w=[f'Xp{j}'])
        for cc in range(NCC):
            j = cc % 2
            S.dma('sp', X[j][:, 3:3 + W], self.XT[cc * 128:(cc + 1) * 128, :], w=[f'X{j}'])
            S.op('dve', lambda e, j=j, cc=cc: e.tensor_scalar(out=acc[j], in0=X[j][:, 0:W], scalar1=cw[:, cc * 5:cc * 5 + 1],
                                                             scalar2=None, op0=ALU.mult),
                 r=[f'X{j}', f'Xp{j}', 'cw'], w=[f'acc{j}'])
            for k in range(1, 4):
                S.op('dve', lambda e, j=j, cc=cc, k=k: e.scalar_tensor_tensor(
                    out=acc[j], in0=X[j][:, k:W + k], scalar=cw[:, cc * 5 + k:cc * 5 + k + 1], in1=acc[j],
                    op0=ALU.mult, op1=ALU.add), r=[f'X{j}', f'Xp{j}', 'cw', f'acc{j}'], w=[f'acc{j}'])
            S.op('act', lambda e, j=j, cc=cc: e.activation(out=ob[j], in_=acc[j], func=AF.Silu, bias=cw[:, cc * 5 + 4:cc * 5 + 5]),
                 r=[f'acc{j}', 'cw'], w=[f'ob{j}'])
            S.dma('sp', self.XC[cc * 128:(cc + 1) * 128, :], ob[j], r=[f'ob{j}'])

    def phase_C(self):
        cfg, S = self.cfg, self.S
        W, T, KV, G, AH, NTO, NTW = cfg.W, cfg.T, cfg.KV, cfg.G, cfg.AH, cfg.NTO, cfg.NTW
        self.reset()
        B_ = self.banks
        GW = G * 128
        kit = self.alloc(W, BF16, parts=64)
        S.dma('sp', kit, self.KiT, w=['kit'])
        kt = self.alloc(KV * W, BF16).rearrange("p (k w) -> p k w", w=W)
        S.dma('sp', kt, self.KT.rearrange("(k p) w -> p k w", p=128), w=['kt'])
        vt = self.alloc(NTW * KV * 128, BF16).rearrange("p (n c) -> p n c", c=KV * 128)
        S.dma('sp', vt, self.V.rearrange("(n p) c -> p n c", p=128), w=['vt'])
        kposb = self.alloc(W)
        S.dma('sp', kposb, self.kpos, w=['kpos'])
        score = self.alloc(W)
        work = self.alloc(W)
        mbias = [self.alloc(512) for _ in range(2)]
        mask = self.alloc(W, BF16)
        maskT = self.alloc(NTW * 128, BF16).rearrange("p (n t) -> p n t", t=128)
        qi = [self.alloc(16 * 128, BF16, parts=64).rearrange("p (h t) -> p h t", t=128) for _ in range(2)]
        qt = [self.alloc(AH * 128, BF16).rearrange("p (a t) -> p a t", t=128) for _ in range(2)]
        qg = self.alloc(16 * 128, BF16, parts=64)
        A = self.alloc(128)
        AT = self.alloc(128, BF16)
        wind = self.alloc(16 * 128, BF16).rearrange("p (j c) -> p j c", c=128)
        Rt = [self.alloc(512, BF16) for _ in range(3)]
        Et = [self.alloc(512, BF16) for _ in range(2)]
        Pt = [self.alloc(512, BF16) for _ in range(2)]
        mx8 = self.alloc(8)
        thr = self.alloc(2)
        rden = self.alloc(512)
        ost = [self.alloc(512, BF16) for _ in range(2)]
        Wi3 = self.Wi_sb.rearrange("p (n h) -> p n h", h=16)
        QiTv = self.QiT.rearrange("(h d) t -> d h t", d=64)
        QTv = self.QT.rearrange("(a p) t -> p a t", p=128)
        MixTv = self.MixT.rearrange("(a p) t -> p a t", p=128)
        b3b = B_[3][:, 0:256].bitcast(BF16)
        scale = 128.0 ** -0.5
        cnt = {'r': 0, 'e': 0, 'l': 0, 'o': 0, 'm': 0}
        for qb in range(NTO):
            j = qb % 2
            nkb = NTO + qb + 1
            Sq = nkb * 128
            nsc = (Sq + 511) // 512
            S.dma('sp', qi[j], QiTv[:, :, qb * 128:(qb + 1) * 128], w=[f'qi{j}'])
            S.dma('sp', qt[j], QTv[:, :, qb * 128:(qb + 1) * 128], w=[f'qt{j}'])
            S.op('pool', lambda e: e.tensor_copy(out=qg.rearrange("p (j h t) -> p j h t", h=16, t=8),
                                                 in_=qi[j].rearrange("p h (j t) -> p j h t", t=8)), r=[f'qi{j}'], w=['qg'])
            S.op('dve', lambda e, qb=qb: e.tensor_tensor(out=A.rearrange("p (h t) -> p h t", t=8), in0=self.mask8.rearrange("p (h t) -> p h t", t=8),
                                                        in1=Wi3[:, qb, :].unsqueeze(2).to_broadcast([128, 16, 8]), op=ALU.mult),
                 r=['Wi', 'cf'], w=['A'])
            S.op('pe', lambda e: e.transpose(out=B_[3][:, 0:128], in_=A, identity=self.ident_f), r=['A', 'cf'], w=['bank3'])
            S.op('act', lambda e: e.copy(out=AT, in_=B_[3][:, 0:128]), r=['bank3'], w=['AT'])
            S.op('pool', lambda e: e.tensor_tensor(out=wind, in0=self.CM.rearrange("p (j c) -> p j c", c=128),
                                                  in1=AT.unsqueeze(1).to_broadcast([128, 16, 128]), op=ALU.mult), r=['AT', 'cb'], w=['wind'])
            for sc in range(nsc):
                ncol = min(512, Sq - sc * 512)
                for jg in range(16):
                    db = jg % 2
                    S.op('pe', lambda e, db=db, jg=jg, sc=sc, ncol=ncol, j=j: e.matmul(
                        B_[db][:, 0:ncol], qg[:, jg * 128:(jg + 1) * 128], kit[:, sc * 512:sc * 512 + ncol], start=True, stop=True),
                        r=['qg', 'kit'], w=[f'bank{db}'])
                    ri = cnt['r'] % 3
                    cnt['r'] += 1
                    R = Rt[ri]
                    if ri % 2 == 0:
                        S.op('act', lambda e, R=R, db=db, ncol=ncol: e.activation(out=R[:, 0:ncol], in_=B_[db][:, 0:ncol], func=AF.Relu),
                             r=[f'bank{db}'], w=[f'R{ri}'])
                    else:
                        S.op('dve', lambda e, R=R, db=db, ncol=ncol: e.tensor_scalar(out=R[:, 0:ncol], in0=B_[db][:, 0:ncol], scalar1=0.0, scalar2=None, op0=ALU.max),
                             r=[f'bank{db}'], w=[f'R{ri}'])
                    S.op('pe', lambda e, R=R, jg=jg, ncol=ncol: e.matmul(B_[2][:, 0:ncol], wind[:, jg, :], R[:, 0:ncol], start=(jg == 0), stop=(jg == 15)),
                         r=[f'R{ri}', 'wind'], w=['bank2'], inc=(jg == 15))
                mb = mbias[sc % 2]
                S.op('dve', lambda e, mb=mb, sc=sc, ncol=ncol, qb=qb: e.tensor_scalar(
                    out=mb[:, 0:ncol], in0=kposb[:, sc * 512:sc * 512 + ncol], scalar1=self.qpos[:, qb:qb + 1], scalar2=-1e30,
                    op0=ALU.is_gt, op1=ALU.mult), r=['kpos', 'cf'], w=[f'mb{sc % 2}'])
                S.op('dve', lambda e, mb=mb, sc=sc, ncol=ncol: e.tensor_tensor(out=score[:, sc * 512:sc * 512 + ncol], in0=B_[2][:, 0:ncol], in1=mb[:, 0:ncol], op=ALU.add),
                     r=['bank2', f'mb{sc % 2}'], w=['score'])
            cur = score[:, 0:Sq]
            ck = 'score'
            nr = cfg.TOPK // 8
            for r_ in range(nr):
                S.op('dve', lambda e, cur=cur: e.max(out=mx8, in_=cur), r=[ck], w=['mx8'])
                if r_ < nr - 1:
                    S.op('dve', lambda e, cur=cur: e.match_replace(out=work[:, 0:Sq], in_to_replace=mx8, in_values=cur, imm_value=-1e30),
                         r=[ck, 'mx8'], w=['work'])
                    cur = work[:, 0:Sq]
                    ck = 'work'
            S.op('dve', lambda e: e.tensor_scalar(out=thr[:, 0:1], in0=mx8[:, 7:8], scalar1=-1e29, scalar2=None, op0=ALU.max), r=['mx8'], w=['thr'])
            S.op('dve', lambda e: e.tensor_scalar(out=mask[:, 0:Sq], in0=score[:, 0:Sq], scalar1=thr[:, 0:1], scalar2=None, op0=ALU.is_ge),
                 r=['score', 'thr'], w=['mask'])
            for kb4 in range((nkb + 3) // 4):
                n = min(4, nkb - kb4 * 4)
                for i in range(n):
                    kb = kb4 * 4 + i
                    S.op('pe', lambda e, i=i, kb=kb: e.transpose(out=b3b[:, i * 128:(i + 1) * 128], in_=mask[:, kb * 128:(kb + 1) * 128], identity=self.ident_b),
                         r=['mask', 'cb'], w=['bank3'], inc=(i == n - 1))
                cnt['m'] += 1
                if cnt['m'] % 2 == 0:
                    S.op('act', lambda e, kb4=kb4, n=n: e.copy(out=maskT[:, kb4 * 4:kb4 * 4 + n, :], in_=b3b[:, 0:n * 128].rearrange("p (n t) -> p n t", t=128)),
                         r=['bank3'], w=['maskT'])
                else:
                    S.op('dve', lambda e, kb4=kb4, n=n: e.tensor_copy(out=maskT[:, kb4 * 4:kb4 * 4 + n, :], in_=b3b[:, 0:n * 128].rearrange("p (n t) -> p n t", t=128)),
                         r=['bank3'], w=['maskT'])
            for k in range(KV):
                for kb in range(nkb):
                    lb = 4 + cnt['l'] % 2
                    cnt['l'] += 1
                    ei = cnt['e'] % 2
                    cnt['e'] += 1
                    E_, P_ = Et[ei], Pt[ei]
                    S.op('pe', lambda e, lb=lb, k=k, kb=kb, j=j: e.matmul(B_[lb][:, 0:GW], kt[:, k, kb * 128:(kb + 1) * 128], qt[j][:, k * G:(k + 1) * G, :],
                                                                      start=True, stop=True), r=['kt', f'qt{j}'], w=[f'bank{lb}'])
                    S.op('act', lambda e, lb=lb, E_=E_: e.activation(out=E_[:, 0:GW], in_=B_[lb][:, 0:GW], func=AF.Exp, scale=scale),
                         r=[f'bank{lb}'], w=[f'E{ei}'])
                    eng = 'dve' if ei == 0 else 'pool'
                    S.op(eng, lambda e, E_=E_, P_=P_, kb=kb: e.tensor_tensor(out=P_[:, 0:GW].rearrange("p (g t) -> p g t", t=128),
                                                                            in0=E_[:, 0:GW].rearrange("p (g t) -> p g t", t=128),
                                                                            in1=maskT[:, kb, :].unsqueeze(1).to_broadcast([128, G, 128]), op=ALU.mult),
                         r=[f'E{ei}', 'maskT'], w=[f'P{ei}'])
                    S.op('pe', lambda e, P_=P_, k=k, kb=kb: e.matmul(B_[6][:, 0:GW], vt[:, kb, k * 128:(k + 1) * 128], P_[:, 0:GW], start=(kb == 0), stop=(kb == nkb - 1)),
                         r=[f'P{ei}', 'vt'], w=['bank6'], inc=False)
                    S.op('pe', lambda e, P_=P_, kb=kb: e.matmul(B_[7][:, 0:GW], self.ones_b, P_[:, 0:GW], start=(kb == 0), stop=(kb == nkb - 1)),
                         r=[f'P{ei}', 'cb'], w=['bank7'])
                oi = cnt['o'] % 2
                cnt['o'] += 1
                S.op('dve', lambda e: e.reciprocal(out=rden[:, 0:GW], in_=B_[7][:, 0:GW]), r=['bank7'], w=['rden'])
                S.op('dve', lambda e, oi=oi: e.tensor_tensor(out=ost[oi][:, 0:GW], in0=B_[6][:, 0:GW], in1=rden[:, 0:GW], op=ALU.mult),
                     r=['bank6', 'rden'], w=[f'ost{oi}'])
                S.dma('sp', MixTv[:, k * G:(k + 1) * G, qb * 128:(qb + 1) * 128], ost[oi][:, 0:GW].rearrange("p (g t) -> p g t", t=128), r=[f'ost{oi}'])

    def phase_D(self):
        cfg, S = self.cfg, self.S
        W, T, SH, SG, SW, XBC = cfg.W, cfg.T, cfg.SH, cfg.SG, cfg.SW, cfg.XBC
        NCC = XBC // 128
        ccB = SW // 128
        ccC = ccB + SG
        NTO, NTW = cfg.NTO, cfg.NTW
        self.reset()
        B_ = self.banks
        bA, bT, bC, bS, bY, bO, bH, bG = 0, 1, 2, 3, 4, 5, 6, 7
        dtb = self.alloc(2 * SH)
        S.dma('sp', dtb, self.c_dtb, w=['dtb'])
        a_b = self.alloc(SH)
        S.op('act', lambda e: e.activation(out=a_b, in_=dtb[:, SH:2 * SH], func=AF.Exp), r=['dtb'], w=['a_e'])
        S.op('dve', lambda e: e.tensor_scalar(out=a_b, in0=a_b, scalar1=-1.0, scalar2=None, op0=ALU.mult), r=['a_e'], w=['a_b'])
        dvec = self.alloc(SW)
        normg = self.alloc(SW)
        S.dma('sp', dvec, self.c_dvec, w=['dvec'])
        S.dma('sp', normg, self.c_normg, w=['normg'])
        H = self.alloc(SW)
        Hb = self.alloc(SW, BF16)
        S.op('dve', lambda e: e.memset(H, 0.0), w=[f'H{g}' for g in range(SG)])
        S.op('pool', lambda e: e.memset(Hb, 0.0), w=[f'Hb{g}' for g in range(SG)])
        H3 = H.rearrange("p (h q) -> p h q", q=64)
        xc = [self.alloc(NCC * 128, BF16).rearrange("p (n l) -> p n l", l=128) for _ in range(2)]
        zt = [self.alloc(SW) for _ in range(2)]
        dt3 = self.dt_sb.rearrange("p (n h) -> p n h", h=SH)
        sm = self.alloc(8 * SH)
        da, acs, diff, decay, eacs, cdb, nacs, dtd = [sm[:, i * SH:(i + 1) * SH] for i in range(8)]
        xs_tm = self.alloc(SW, BF16)
        xs3 = xs_tm.rearrange("p (h q) -> p h q", q=64)
        B_tm = self.alloc(SG * 128, BF16).rearrange("p (g n) -> p g n", n=128)
        xdt = self.alloc(SW, BF16)
        xdt3 = xdt.rearrange("p (h q) -> p h q", q=64)
        xdtd = self.alloc(SW, BF16)
        xdtd3 = xdtd.rearrange("p (h q) -> p h q", q=64)
        R1 = self.alloc(512)
        R13 = R1.rearrange("p (j l) -> p j l", l=128)
        Lm = self.alloc(512)
        Lm3 = Lm.rearrange("p (j l) -> p j l", l=128)
        M = self.alloc(512, BF16)
        M3 = M.rearrange("p (j l) -> p j l", l=128)
        t1 = self.alloc(256)
        t2 = self.alloc(256)
        tH = self.alloc(256)
        y = self.alloc(SW)
        sz = self.alloc(SW)
        gy = self.alloc(SW)
        junk = self.alloc(256)
        ss = self.alloc(4 * SG)
        go = self.alloc(SW, BF16)
        ostT = [self.alloc(512, BF16) for _ in range(2)]
        XCv = self.XC.rearrange("(n p) w -> p n w", p=128)
        MixTv = self.MixT.rearrange("(a p) t -> p a t", p=128)
        bTb = B_[bT][:, 0:256].bitcast(BF16)
        bGb = B_[bG][:, 0:256].bitcast(BF16)
        cp = [0]

        def copy_any(out, in_, r, w):
            cp[0] += 1
            if cp[0] % 2 == 0:
                S.op('act', lambda e: e.copy(out=out, in_=in_), r=r, w=w)
            else:
                S.op('dve', lambda e: e.tensor_copy(out=out, in_=in_), r=r, w=w)

        for c in range(NTW):
            own = c >= NTO
            oc = c - NTO
            j = c % 2
            xcj = xc[j]
            xk = f'xc{j}'
            S.dma('sp', xcj, XCv[:, :, c * 128:(c + 1) * 128], w=[xk])
            if own:
                S.dma('sp', zt[j], self.Z[oc * 128:(oc + 1) * 128, :], w=[f'zt{j}'])
            if c == NTO:
                S.op('dve', lambda e: e.tensor_scalar(out=H, in0=H, scalar1=self.flag_sb[:, 0:1], scalar2=None, op0=ALU.mult),
                     r=[f'H{g}' for g in range(SG)] + ['flag'], w=[f'H{g}' for g in range(SG)])
                S.op('act', lambda e: e.copy(out=Hb, in_=H), r=[f'H{g}' for g in range(SG)], w=[f'Hb{g}' for g in range(SG)])
            dtc = dt3[:, c, :]
            S.op('dve', lambda e, dtc=dtc: e.tensor_tensor(out=da, in0=dtc, in1=a_b, op=ALU.mult), r=['dt', 'a_b'], w=['da'])
            S.op('pe', lambda e: e.matmul(B_[bA][:, 0:SH], self.triU_f, da, start=True, stop=True), r=['da', 'cf'], w=['bA'])
            S.op('pe', lambda e: e.matmul(B_[bA][:, 64:64 + SH], self.ones_f, da, start=True, stop=True), r=['da', 'cf'], w=['bA'])
            S.op('act', lambda e: e.copy(out=acs, in_=B_[bA][:, 0:SH]), r=['bA'], w=['acs'])
            S.op('dve', lambda e: e.tensor_tensor(out=diff, in0=B_[bA][:, 64:64 + SH], in1=acs, op=ALU.subtract), r=['bA', 'acs'], w=['diff'])
            S.op('act', lambda e: e.activation(out=decay, in_=diff, func=AF.Exp), r=['diff'], w=['decay'])
            S.op('act', lambda e: e.activation(out=eacs, in_=acs, func=AF.Exp), r=['acs'], w=['eacs'])
            S.op('act', lambda e: e.activation(out=cdb, in_=B_[bA][:, 64:64 + SH], func=AF.Exp), r=['bA'], w=['cdb'])
            S.op('dve', lambda e: e.tensor_scalar(out=nacs, in0=acs, scalar1=-1.0, scalar2=None, op0=ALU.mult), r=['acs'], w=['nacs'])
            S.op('dve', lambda e, dtc=dtc: e.tensor_tensor(out=dtd, in0=dtc, in1=decay, op=ALU.mult), r=['dt', 'decay'], w=['dtd'])
            for q4 in range(ccB // 4):
                for i in range(4):
                    S.op('pe', lambda e, q4=q4, i=i, xcj=xcj: e.transpose(out=bTb[:, i * 128:(i + 1) * 128], in_=xcj[:, q4 * 4 + i, :],
                                                                        identity=self.ident_b), r=[xk, 'cb'], w=['bT'], inc=(i == 3))
                copy_any(xs_tm[:, q4 * 512:(q4 + 1) * 512], bTb[:, 0:512], r=['bT'], w=['xs'])
            for g4 in range((SG + 3) // 4):
                ng = min(4, SG - g4 * 4)
                for i in range(ng):
                    S.op('pe', lambda e, g4=g4, i=i, xcj=xcj: e.transpose(out=bTb[:, i * 128:(i + 1) * 128], in_=xcj[:, ccB + g4 * 4 + i, :],
                                                                        identity=self.ident_b), r=[xk, 'cb'], w=['bT'], inc=(i == ng - 1))
                copy_any(B_tm[:, g4 * 4:g4 * 4 + ng, :], bTb[:, 0:ng * 128].rearrange("p (g n) -> p g n", n=128), r=['bT'], w=['Btm'])
            S.op('dve', lambda e, dtc=dtc: e.tensor_tensor(out=xdt3, in0=xs3, in1=dtc.unsqueeze(2).to_broadcast([128, SH, 64]), op=ALU.mult),
                 r=['xs', 'dt'], w=['xdt'])
            S.op('pool', lambda e: e.tensor_tensor(out=xdtd3, in0=xs3, in1=dtd.unsqueeze(2).to_broadcast([128, SH, 64]), op=ALU.mult),
                 r=['xs', 'dtd'], w=['xdtd'])
            if own:
                for g in range(SG):
                    gs = slice(g * 256, (g + 1) * 256)
                    hs = slice(4 * g, 4 * g + 4)
                    S.op('pe', lambda e, g=g, xcj=xcj: e.matmul(B_[bC][:, 0:128], xcj[:, ccB + g, :], xcj[:, ccC + g, :], start=True, stop=True),
                         r=[xk], w=['bC'])
                    S.op('pool', lambda e, hs=hs: e.tensor_tensor(out=R13, in0=self.triU_f.unsqueeze(1).to_broadcast([128, 4, 128]),
                                                                 in1=da[:, hs].unsqueeze(2).to_broadcast([128, 4, 128]), op=ALU.mult),
                         r=['da', 'cf'], w=['R1'])
                    S.op('pe', lambda e: e.matmul(B_[bS][:, 0:512], self.ones_f, R1, start=True, stop=False), r=['R1', 'cf'], w=['bS'], inc=False)
                    S.op('pe', lambda e: e.matmul(B_[bS][:, 0:512], self.ident_b, self.negmask4, start=False, stop=True), r=['cb'], w=['bS'])
                    for jj in range(4):
                        S.op('act', lambda e, jj=jj, g=g: e.activation(out=Lm3[:, jj, :], in_=B_[bS][:, jj * 128:(jj + 1) * 128], func=AF.Exp,
                                                                      bias=nacs[:, 4 * g + jj:4 * g + jj + 1]), r=['bS', 'nacs'], w=[f'Lm{jj}'])
                    S.op('dve', lambda e: e.tensor_tensor(out=M3, in0=Lm3, in1=B_[bC][:, 0:128].unsqueeze(1).to_broadcast([128, 4, 128]), op=ALU.mult),
                         r=['bC'] + [f'Lm{jj}' for jj in range(4)], w=['M'])
                    for jj in range(4):
                        S.op('pe', lambda e, jj=jj, g=g: e.matmul(B_[bY][:, jj * 64:(jj + 1) * 64], M3[:, jj, :], xdt3[:, 4 * g + jj, :], start=True, stop=True),
                             r=['M', 'xdt'], w=['bY'], inc=(jj == 3))
                    S.op('pe', lambda e, g=g, gs=gs, xcj=xcj: e.matmul(B_[bO][:, 0:256], xcj[:, ccC + g, :], Hb[:, gs], start=True, stop=True),
                         r=[xk, f'Hb{g}'], w=['bO'])
                    S.op('dve', lambda e, hs=hs: e.tensor_tensor(out=t1.rearrange("p (j q) -> p j q", q=64),
                                                                in0=B_[bO][:, 0:256].rearrange("p (j q) -> p j q", q=64),
                                                                in1=eacs[:, hs].unsqueeze(2).to_broadcast([128, 4, 64]), op=ALU.mult),
                         r=['bO', 'eacs'], w=['t1'])
                    S.op('pool', lambda e, gs=gs: e.tensor_tensor(out=t2, in0=xs_tm[:, gs], in1=dvec[:, gs], op=ALU.mult), r=['xs', 'dvec'], w=['t2'])
                    S.op('pool', lambda e: e.tensor_tensor(out=t2, in0=t2, in1=t1, op=ALU.add), r=['t1', 't2'], w=['t2'])
                    S.op('dve', lambda e, gs=gs: e.tensor_tensor(out=y[:, gs], in0=B_[bY][:, 0:256], in1=t2, op=ALU.add), r=['bY', 't2'], w=[f'y{g}'])
            for g in range(SG):
                gs = slice(g * 256, (g + 1) * 256)
                hs = slice(4 * g, 4 * g + 4)
                S.op('pe', lambda e, g=g, hs=hs: e.matmul(B_[bH][:, 0:256], B_tm[:, g, :], xdtd3[:, hs, :], start=True, stop=True),
                     r=['Btm', 'xdtd'], w=['bH'])
                S.op('pool', lambda e, hs=hs: e.tensor_tensor(out=tH.rearrange("p (j q) -> p j q", q=64), in0=H3[:, hs, :],
                                                             in1=cdb[:, hs].unsqueeze(2).to_broadcast([128, 4, 64]), op=ALU.mult),
                     r=[f'H{g}', 'cdb'], w=['tH'])
                S.op('dve', lambda e, gs=gs: e.tensor_tensor(out=H[:, gs], in0=B_[bH][:, 0:256], in1=tH, op=ALU.add), r=['bH', 'tH'], w=[f'H{g}'])
                S.op('act', lambda e, gs=gs: e.copy(out=Hb[:, gs], in_=H[:, gs]), r=[f'H{g}'], w=[f'Hb{g}'])
            if own:
                yk = [f'y{g}' for g in range(SG)]
                S.op('act', lambda e, j=j: e.activation(out=sz, in_=zt[j], func=AF.Silu), r=[f'zt{j}'], w=['sz'])
                S.op('dve', lambda e: e.tensor_tensor(out=gy, in0=y, in1=sz, op=ALU.mult), r=yk + ['sz'], w=['gy'])
                for g in range(SG):
                    S.op('act', lambda e, g=g: e.activation(out=junk, in_=gy[:, g * 256:(g + 1) * 256], func=AF.Square, accum_out=ss[:, g:g + 1]),
                         r=['gy'], w=['junk', f'ss{g}'])
                ssk = [f'ss{g}' for g in range(SG)]
                S.op('act', lambda e: e.activation(out=ss[:, SG:2 * SG], in_=ss[:, 0:SG], func=AF.Sqrt, scale=1.0 / 256, bias=self.eps5[:, 0:1]),
                     r=ssk + ['eps5'], w=['sd'])
                S.op('dve', lambda e: e.reciprocal(out=ss[:, 2 * SG:3 * SG], in_=ss[:, SG:2 * SG]), r=['sd'], w=['rs'])
                S.op('dve', lambda e: e.tensor_tensor(out=gy.rearrange("p (g q) -> p g q", q=256), in0=gy.rearrange("p (g q) -> p g q", q=256),
                                                      in1=ss[:, 2 * SG:3 * SG].unsqueeze(2).to_broadcast([128, SG, 256]), op=ALU.mult),
                     r=['gy', 'rs'], w=['gy'])
                S.op('pool', lambda e: e.tensor_tensor(out=go, in0=gy, in1=normg, op=ALU.mult), r=['gy', 'normg'], w=['go'])
                for f4 in range(SW // 512):
                    o = ostT[f4 % 2]
                    ok = f'ostT{f4 % 2}'
                    for i in range(4):
                        S.op('pe', lambda e, f4=f4, i=i: e.transpose(out=bGb[:, i * 128:(i + 1) * 128], in_=go[:, (f4 * 4 + i) * 128:(f4 * 4 + i + 1) * 128],
                                                                    identity=self.ident_b), r=['go', 'cb'], w=['bG'], inc=(i == 3))
                    copy_any(o, bGb[:, 0:512], r=['bG'], w=[ok])
                    S.dma('sp', MixTv[:, cfg.AH + f4 * 4:cfg.AH + f4 * 4 + 4, oc * 128:(oc + 1) * 128],
                          o.rearrange("p (a t) -> p a t", t=128), r=[ok])

    def layer_norm(self, xt, xk, junk, st, lng, lnb, tag):
        S, D = self.S, self.cfg.D
        S.op('act', lambda e: e.activation(out=junk, in_=xt, func=AF.Identity, accum_out=st[:, 0:1]), r=[xk], w=['junk', tag + 's1'])
        S.op('act', lambda e: e.activation(out=junk, in_=xt, func=AF.Square, accum_out=st[:, 1:2]), r=[xk], w=['junk', tag + 's2'])
        S.op('dve', lambda e: e.tensor_scalar(out=st[:, 2:3], in0=st[:, 0:1], scalar1=1.0 / D, scalar2=None, op0=ALU.mult), r=[tag + 's1'], w=[tag + 'mean'])
        S.op('dve', lambda e: e.tensor_tensor(out=st[:, 3:4], in0=st[:, 2:3], in1=st[:, 2:3], op=ALU.mult), r=[tag + 'mean'], w=[tag + 'm2'])
        S.op('dve', lambda e: e.scalar_tensor_tensor(out=st[:, 4:5], in0=st[:, 1:2], scalar=1.0 / D, in1=st[:, 3:4], op0=ALU.mult, op1=ALU.subtract),
             r=[tag + 's2', tag + 'm2'], w=[tag + 'var'])
        S.op('act', lambda e: e.activation(out=st[:, 5:6], in_=st[:, 4:5], func=AF.Sqrt, bias=self.eps5[:, 0:1]), r=[tag + 'var', 'eps5'], w=[tag + 'sd'])
        S.op('dve', lambda e: e.reciprocal(out=st[:, 6:7], in_=st[:, 5:6]), r=[tag + 'sd'], w=[tag + 'rstd'])
        S.op('dve', lambda e: e.tensor_scalar(out=st[:, 7:8], in0=st[:, 2:3], scalar1=-1.0, scalar2=st[:, 6:7], op0=ALU.mult, op1=ALU.mult),
             r=[tag + 'mean', tag + 'rstd'], w=[tag + 'nmr'])
        S.op('act', lambda e: e.activation(out=xt, in_=xt, func=AF.Identity, scale=st[:, 6:7], bias=st[:, 7:8]), r=[xk, tag + 'rstd', tag + 'nmr'], w=[xk])
        S.op('dve', lambda e: e.tensor_tensor(out=xt, in0=xt, in1=lng, op=ALU.mult), r=[xk, 'lng'], w=[xk])
        S.op('pool', lambda e: e.tensor_tensor(out=xt, in0=xt, in1=lnb, op=ALU.add), r=[xk, 'lnb'], w=[xk])

    def phase_E(self):
        cfg, S = self.cfg, self.S
        D, T, KC, ETOK, NTO, CAP = cfg.D, cfg.T, cfg.KC, cfg.ETOK, cfg.NTO, cfg.CAP
        self.reset()
        B_ = self.banks
        nt = ETOK // 128
        OC = 256
        mixT = self.alloc(KC * ETOK, BF16).rearrange("p (k t) -> p k t", t=ETOK)
        wo = [self.alloc(KC * OC, BF16).rearrange("p (k c) -> p k c", c=OC) for _ in range(2)]
        h1 = [self.alloc(D) for _ in range(nt)]
        lng = self.alloc(D)
        lnb = self.alloc(D)
        S.dma('sp', lng, self.c_ln[0:1, :].partition_broadcast(128)[:, 0, :], w=['lng'])
        S.dma('sp', lnb, self.c_ln[1:2, :].partition_broadcast(128)[:, 0, :], w=['lnb'])
        wr = self.alloc(KC * 72).rearrange("p (k c) -> p k c", c=72)
        S.dma('sp', wr, self.w_r.rearrange("(k p) c -> p k c", p=128), w=['wr'])
        rb = self.alloc(72)
        S.dma('sp', rb, self.c_rb, w=['rb'])
        x1T = self.alloc(KC * 128).rearrange("p (k t) -> p k t", t=128)
        x1b = [self.alloc(D, BF16) for _ in range(2)]
        junk = self.alloc(D)
        st = self.alloc(8)
        zero = self.alloc(D, BF16)
        lg = self.alloc(72)
        rt = self.alloc(32)
        ohg = self.alloc(8)
        pen = self.alloc(8)
        lem = self.alloc(64)
        oh1 = self.alloc(64)
        oh2 = self.alloc(64)
        tmp = self.alloc(64)
        ohs = self.alloc(64, BF16)
        pos = self.alloc(64)
        MixTv = self.MixT.rearrange("(k p) t -> p k t", p=128)
        w_out_v = self.w_out.rearrange("(k p) c -> p k c", p=128)
        S.op('pool', lambda e: e.memset(zero, 0.0), w=['zero'])
        S.op('dve', lambda e: e.memset(self.cnt_b, 0.0), w=['cnt'])
        for blk in range(cfg.E * CAP // 128):
            S.dma('sp', self.XE[blk * 128:(blk + 1) * 128, :], zero, r=['zero'], w=['XE'])
        cnt = {'b': 0, 'w': 0}
        for tg in range(T // ETOK):
            S.dma('sp', mixT, MixTv[:, :, tg * ETOK:(tg + 1) * ETOK], w=['mixT'])
            for i in range(nt):
                r0 = tg * ETOK + i * 128
                S.dma('sp', h1[i], self.x_own[r0:r0 + 128, :], w=[f'h1_{i}'])
            for oc in range(D // OC):
                wj = cnt['w'] % 2
                cnt['w'] += 1
                S.dma('pool', wo[wj], w_out_v[:, :, oc * OC:(oc + 1) * OC], w=[f'wo{wj}'])
                for i in range(nt):
                    b = cnt['b'] % 4
                    cnt['b'] += 1
                    for kc in range(KC):
                        S.op('pe', lambda e: e.matmul(B_[b][:, 0:OC], mixT[:, kc, i * 128:(i + 1) * 128], wo[wj][:, kc, :], start=(kc == 0), stop=(kc == KC - 1)),
                             r=['mixT', f'wo{wj}'], w=[f'bank{b}'], inc=(kc == KC - 1))
                    S.op('dve', lambda e: e.scalar_tensor_tensor(out=h1[i][:, oc * OC:(oc + 1) * OC], in0=h1[i][:, oc * OC:(oc + 1) * OC], scalar=cfg.alpha,
                                                                 in1=B_[b][:, 0:OC], op0=ALU.mult, op1=ALU.add), r=[f'bank{b}', f'h1_{i}'], w=[f'h1_{i}'])
            for i in range(nt):
                ot = tg * nt + i
                r0 = ot * 128
                xk = f'h1_{i}'
                xt = h1[i]
                self.layer_norm(xt, xk, junk, st, lng, lnb, 'l1')
                xb = x1b[ot % 2]
                xbk = f'x1b{ot % 2}'
                S.op('act', lambda e: e.copy(out=xb, in_=xt), r=[xk], w=[xbk])
                S.dma('sp', self.X1[r0:r0 + 128, :], xt, r=[xk])
                for k4 in range(KC // 4):
                    for ii in range(4):
                        S.op('pe', lambda e: e.transpose(out=B_[4][:, ii * 128:(ii + 1) * 128], in_=xt[:, (k4 * 4 + ii) * 128:(k4 * 4 + ii + 1) * 128],
                                                         identity=self.ident_f), r=[xk, 'cf'], w=['bank4'], inc=(ii == 3))
                    if k4 % 2 == 0:
                        S.op('act', lambda e: e.copy(out=x1T[:, k4 * 4:k4 * 4 + 4, :], in_=B_[4][:, 0:512].rearrange("p (k t) -> p k t", t=128)), r=['bank4'], w=['x1T'])
                    else:
                        S.op('dve', lambda e: e.tensor_copy(out=x1T[:, k4 * 4:k4 * 4 + 4, :], in_=B_[4][:, 0:512].rearrange("p (k t) -> p k t", t=128)), r=['bank4'], w=['x1T'])
                for kc in range(KC):
                    S.op('pe', lambda e: e.matmul(B_[5][:, 0:72], x1T[:, kc, :], wr[:, kc, :], start=(kc == 0), stop=(kc == KC - 1)),
                         r=['x1T', 'wr'], w=['bank5'], inc=(kc == KC - 1))
                S.op('dve', lambda e: e.tensor_tensor(out=lg, in0=B_[5][:, 0:72], in1=rb, op=ALU.add), r=['bank5', 'rb'], w=['lg'])
                gmax, ngmax, gs, gw, e1, e2, dd, ex, p1, g1, g2, s1f, s2f = [rt[:, c:c + 1] for c in range(13)]
                S.op('dve', lambda e: e.tensor_reduce(out=gmax, in_=lg[:, 0:8], axis=AX.X, op=ALU.max), r=['lg'], w=['gmax'])
                S.op('dve', lambda e: e.tensor_scalar(out=ngmax, in0=gmax, scalar1=-1.0, scalar2=None, op0=ALU.mult), r=['gmax'], w=['ngmax'])
                S.op('act', lambda e: e.activation(out=tmp[:, 0:8], in_=lg[:, 0:8], func=AF.Exp, bias=ngmax, accum_out=gs), r=['lg', 'ngmax'], w=['tmp', 'gs'])
                S.op('dve', lambda e: e.reciprocal(out=gw, in_=gs), r=['gs'], w=['gw'])
                S.op('dve', lambda e: e.tensor_scalar(out=ohg, in0=lg[:, 0:8], scalar1=gmax, scalar2=None, op0=ALU.is_equal), r=['lg', 'gmax'], w=['ohg'])
                S.op('dve', lambda e: e.tensor_scalar(out=pen, in0=ohg, scalar1=-1.0, scalar2=1e30, op0=ALU.add, op1=ALU.mult), r=['ohg'], w=['pen'])
                S.op('dve', lambda e: e.tensor_tensor(out=lem.rearrange("p (g x) -> p g x", x=8), in0=lg[:, 8:72].rearrange("p (g x) -> p g x", x=8),
                                                      in1=pen.unsqueeze(2).to_broadcast([128, 8, 8]), op=ALU.add), r=['lg', 'pen'], w=['lem'])
                S.op('dve', lambda e: e.tensor_reduce(out=e1, in_=lem, axis=AX.X, op=ALU.max), r=['lem'], w=['e1'])
                S.op('dve', lambda e: e.tensor_scalar(out=oh1, in0=lem, scalar1=e1, scalar2=None, op0=ALU.is_equal), r=['lem', 'e1'], w=['oh1'])
                S.op('dve', lambda e: e.scalar_tensor_tensor(out=lem, in0=oh1, scalar=-1e30, in1=lem, op0=ALU.mult, op1=ALU.add), r=['oh1', 'lem'], w=['lem'])
                S.op('dve', lambda e: e.tensor_reduce(out=e2, in_=lem, axis=AX.X, op=ALU.max), r=['lem'], w=['e2'])
                S.op('dve', lambda e: e.tensor_scalar(out=oh2, in0=lem, scalar1=e2, scalar2=None, op0=ALU.is_equal), r=['lem', 'e2'], w=['oh2'])
                S.op('dve', lambda e: e.tensor_tensor(out=dd, in0=e2, in1=e1, op=ALU.subtract), r=['e1', 'e2'], w=['dd'])
                S.op('act', lambda e: e.activation(out=ex, in_=dd, func=AF.Exp), r=['dd'], w=['ex'])
                S.op('dve', lambda e: e.tensor_scalar(out=ex, in0=ex, scalar1=1.0, scalar2=None, op0=ALU.add), r=['ex'], w=['ex'])
                S.op('dve', lambda e: e.reciprocal(out=p1, in_=ex), r=['ex'], w=['p1'])
                gcol = self.gates[:, 2 * ot:2 * ot + 2]
                S.op('dve', lambda e: e.tensor_tensor(out=gcol[:, 0:1], in0=gw, in1=p1, op=ALU.mult), r=['gw', 'p1'], w=['gates'])
                S.op('dve', lambda e: e.tensor_tensor(out=gcol[:, 1:2], in0=gw, in1=gcol[:, 0:1], op=ALU.subtract), r=['gw', 'gates'], w=['gates'])
                S.op('dve', lambda e: e.tensor_tensor(out=ohs, in0=oh1, in1=oh2, op=ALU.add), r=['oh1', 'oh2'], w=['ohs'])
                S.op('pe', lambda e: e.matmul(B_[6][:, 0:64], self.stri_b, ohs, start=True, stop=True), r=['ohs', 'cb'], w=['bank6'])
                S.op('pe', lambda e: e.matmul(B_[7][:, 0:64], self.ones_b, ohs, start=True, stop=True), r=['ohs', 'cb'], w=['bank7'])
                S.op('dve', lambda e: e.tensor_tensor(out=pos, in0=B_[6][:, 0:64], in1=self.cnt_b, op=ALU.add), r=['bank6', 'cnt'], w=['pos'])
                S.op('dve', lambda e: e.tensor_tensor(out=self.cnt_b, in0=B_[7][:, 0:64], in1=self.cnt_b, op=ALU.add), r=['bank7', 'cnt', 'pos'], w=['cnt'])
                S.op('dve', lambda e: e.tensor_scalar(out=pos, in0=pos, scalar1=float(CAP - 1), scalar2=None, op0=ALU.min), r=['pos'], w=['pos'])
                S.op('dve', lambda e: e.tensor_tensor(out=pos, in0=pos, in1=self.iota_e, op=ALU.add), r=['pos', 'cf'], w=['pos'])
                S.op('dve', lambda e: e.tensor_tensor(out=tmp, in0=oh1, in1=pos, op=ALU.mult), r=['oh1', 'pos', 'tmp'], w=['tmp'])
                S.op('dve', lambda e: e.tensor_reduce(out=s1f, in_=tmp, axis=AX.X, op=ALU.add), r=['tmp'], w=['s1f'])
                S.op('dve', lambda e: e.tensor_tensor(out=tmp, in0=oh2, in1=pos, op=ALU.mult), r=['oh2', 'pos', 's1f'], w=['tmp'])
                S.op('dve', lambda e: e.tensor_reduce(out=s2f, in_=tmp, axis=AX.X, op=ALU.add), r=['tmp'], w=['s2f'])
                S.op('dve', lambda e: e.tensor_copy(out=self.slots[:, 2 * ot:2 * ot + 1], in_=s1f), r=['s1f'], w=['slots'])
                S.op('dve', lambda e: e.tensor_copy(out=self.slots[:, 2 * ot + 1:2 * ot + 2], in_=s2f), r=['s2f', 'slots'], w=['slots'])
                for c in range(2):
                    S.dma('pool', self.XE, xb, r=[xbk, 'slots', 'XE'],
                          indirect=dict(out_offset=bass.IndirectOffsetOnAxis(ap=self.slots[:, 2 * ot + c:2 * ot + c + 1], axis=0), in_offset=None))

    def phase_F(self):
        cfg, S = self.cfg, self.S
        D, T, KC, DE, E, CAP, NTO = cfg.D, cfg.T, cfg.KC, cfg.DE, cfg.E, cfg.CAP, cfg.NTO
        self.reset()
        B_ = self.banks
        KE = DE // 128
        NCH = max(1, DE // 512)
        CW = min(512, DE)
        PK = min(8, KC)
        NPG = KC // PK
        DP = min(1024, D)
        NPD = D // DP
        xe = [self.alloc(D, BF16) for _ in range(2)]
        xeT = [self.alloc(KC * 128, BF16).rearrange("p (k t) -> p k t", t=128) for _ in range(2)]
        NWP = 4
        wp = [self.alloc(8 * 1024, BF16) for _ in range(NWP)]
        sg = self.alloc(DE)
        hb = self.alloc(DE, BF16)
        hT = self.alloc(KE * 128, BF16).rearrange("p (k t) -> p k t", t=128)
        ye = [self.alloc(D) for _ in range(2)]
        b0b = B_[0][:, 0:256].bitcast(BF16)
        b7b = B_[7][:, 0:256].bitcast(BF16)
        cnt = {'w': 0, 'c': 0}

        def copy_any(out, in_, r, w):
            cnt['c'] += 1
            if cnt['c'] % 2 == 0:
                S.op('act', lambda e: e.copy(out=out, in_=in_), r=r, w=w)
            else:
                S.op('dve', lambda e: e.tensor_copy(out=out, in_=in_), r=r, w=w)

        def load_piece(src_ap, shape3):
            wi = cnt['w'] % NWP
            cnt['w'] += 1
            n = shape3[0] * shape3[1]
            t = wp[wi][:, 0:n].rearrange("p (k c) -> p k c", c=shape3[1])
            S.dma('pool', t, src_ap, w=[f'wp{wi}'])
            return t, f'wp{wi}'

        for ex in range(E):
            j = ex % 2
            S.dma('sp', xe[j], self.XE[ex * CAP:(ex + 1) * CAP, :], w=[f'xe{j}'])
            for k4 in range(KC // 4):
                for i in range(4):
                    S.op('pe', lambda e: e.transpose(out=b0b[:, i * 128:(i + 1) * 128], in_=xe[j][:, (k4 * 4 + i) * 128:(k4 * 4 + i + 1) * 128], identity=self.ident_b),
                         r=[f'xe{j}', 'cb'], w=['bank0'], inc=(i == 3))
                copy_any(xeT[j][:, k4 * 4:k4 * 4 + 4, :], b0b[:, 0:512].rearrange("p (k t) -> p k t", t=128), r=['bank0'], w=[f'xeT{j}'])
            for which, wsrc, b0 in (('g', self.w_gate, 1), ('u', self.w_up, 3)):
                wv = wsrc[ex].rearrange("(k p) c -> p k c", p=128)
                for pi in range(NPG):
                    t, tk = load_piece(wv[:, pi * PK:(pi + 1) * PK, :], (PK, DE))
                    for kc in range(PK):
                        for nch in range(NCH):
                            first = (pi == 0 and kc == 0)
                            last = (pi == NPG - 1 and kc == PK - 1)
                            S.op('pe', lambda e: e.matmul(B_[b0 + nch][:, 0:CW], xeT[j][:, pi * PK + kc, :], t[:, kc, nch * CW:(nch + 1) * CW], start=first, stop=last),
                                 r=[f'xeT{j}', tk], w=[f'bank{b0 + nch}'], inc=(last and nch == NCH - 1))
                for nch in range(NCH):
                    cs = slice(nch * CW, (nch + 1) * CW)
                    if which == 'g':
                        S.op('act', lambda e: e.activation(out=sg[:, cs], in_=B_[b0 + nch][:, 0:CW], func=AF.Silu), r=[f'bank{b0 + nch}'], w=[f'sg{nch}'])
                    else:
                        S.op('dve', lambda e: e.tensor_tensor(out=hb[:, cs], in0=B_[b0 + nch][:, 0:CW], in1=sg[:, cs], op=ALU.mult),
                             r=[f'bank{b0 + nch}', f'sg{nch}'], w=[f'hb{nch}'])
            hbk = [f'hb{nch}' for nch in range(NCH)]
            for k4 in range((KE + 3) // 4):
                n = min(4, KE - k4 * 4)
                for i in range(n):
                    S.op('pe', lambda e: e.transpose(out=b7b[:, i * 128:(i + 1) * 128], in_=hb[:, (k4 * 4 + i) * 128:(k4 * 4 + i + 1) * 128], identity=self.ident_b),
                         r=hbk + ['cb'], w=['bank7'], inc=(i == n - 1))
                copy_any(hT[:, k4 * 4:k4 * 4 + n, :], b7b[:, 0:n * 128].rearrange("p (k t) -> p k t", t=128), r=['bank7'], w=['hT'])
            wdv = self.w_down[ex].rearrange("(k p) c -> p k c", p=128)
            for pi in range(NPD):
                t, tk = load_piece(wdv[:, :, pi * DP:(pi + 1) * DP], (KE, DP))
                for nch in range(DP // 512):
                    b = 5 + nch % 2
                    for kc in range(KE):
                        S.op('pe', lambda e: e.matmul(B_[b][:, 0:512], hT[:, kc, :], t[:, kc, nch * 512:(nch + 1) * 512], start=(kc == 0), stop=(kc == KE - 1)),
                             r=['hT', tk], w=[f'bank{b}'], inc=(kc == KE - 1))
                    c0 = pi * DP + nch * 512
                    copy_any(ye[j][:, c0:c0 + 512], B_[b][:, 0:512], r=[f'bank{b}'], w=[f'ye{j}'])
            S.dma('sp', self.YE[ex * CAP:(ex + 1) * CAP, :], ye[j], r=[f'ye{j}'])
        self.reset()
        lng = self.alloc(D)
        lnb = self.alloc(D)
        S.dma('sp', lng, self.c_ln[2:3, :].partition_broadcast(128)[:, 0, :], w=['lng'])
        S.dma('sp', lnb, self.c_ln[3:4, :].partition_broadcast(128)[:, 0, :], w=['lnb'])
        r1 = [self.alloc(D) for _ in range(2)]
        r2 = [self.alloc(D) for _ in range(2)]
        x1t = [self.alloc(D) for _ in range(2)]
        junk = self.alloc(D)
        st = self.alloc(8)
        for ot in range(NTO):
            j = ot % 2
            r0 = ot * 128
            S.dma('pool', r1[j], self.YE, r=['slots'], w=[f'r1_{j}'],
                  indirect=dict(out_offset=None, in_offset=bass.IndirectOffsetOnAxis(ap=self.slots[:, 2 * ot:2 * ot + 1], axis=0)))
            S.dma('pool', r2[j], self.YE, r=['slots'], w=[f'r2_{j}'],
                  indirect=dict(out_offset=None, in_offset=bass.IndirectOffsetOnAxis(ap=self.slots[:, 2 * ot + 1:2 * ot + 2], axis=0)))
            S.dma('sp', x1t[j], self.X1[r0:r0 + 128, :], w=[f'x1t{j}'])
            xk = f'x1t{j}'
            xt = x1t[j]
            S.op('dve', lambda e: e.tensor_scalar(out=r1[j], in0=r1[j], scalar1=self.gates[:, 2 * ot:2 * ot + 1], scalar2=None, op0=ALU.mult),
                 r=[f'r1_{j}', 'gates'], w=[f'r1_{j}'])
            S.op('dve', lambda e: e.scalar_tensor_tensor(out=r1[j], in0=r2[j], scalar=self.gates[:, 2 * ot + 1:2 * ot + 2], in1=r1[j], op0=ALU.mult, op1=ALU.add),
                 r=[f'r1_{j}', f'r2_{j}', 'gates'], w=[f'r1_{j}'])
            S.op('dve', lambda e: e.scalar_tensor_tensor(out=xt, in0=xt, scalar=cfg.alpha, in1=r1[j], op0=ALU.mult, op1=ALU.add),
                 r=[f'r1_{j}', xk], w=[xk])
            self.layer_norm(xt, xk, junk, st, lng, lnb, 'l2')
            S.dma('sp', self.out[r0:r0 + 128, :], xt, r=[xk])


def _consts(cfg):
    NTO = cfg.NTO
    cf = np.zeros((128, 128 * 3 + NTO + 64 + 128), np.float32)
    cf[:, 0:128] = np.eye(128)
    cf[:, 128:256] = np.triu(np.ones((128, 128)))
    cf[:, 256:384] = 1.0
    cf[:, 384:384 + NTO] = cfg.T + np.arange(NTO)[None, :] * 128 + np.arange(128)[:, None]
    cf[:, 384 + NTO:384 + NTO + 64] = (np.arange(64) * cfg.CAP)[None, :]
    m8 = np.zeros((128, 16, 8), np.float32)
    for t in range(128):
        m8[t, :, t % 8] = 1.0
    cf[:, 384 + NTO + 64:] = m8.reshape(128, 128)
    cb = np.zeros((128, 128 * 3 + 512 + 2048), np.float32)
    cb[:, 0:128] = np.eye(128)
    cb[:, 128:256] = 1.0
    cb[:, 256:384] = np.triu(np.ones((128, 128)), 1)
    nm = np.where(np.arange(128)[None, :] < np.arange(128)[:, None], -30000.0, 0.0)
    cb[:, 384:896] = np.tile(nm, (1, 4))
    cm = np.zeros((128, 16, 128), np.float32)
    for j in range(16):
        cm[:, j, 8 * j:8 * j + 8] = 1.0
    cb[:, 896:] = cm.reshape(128, 2048)
    return cf, cb.astype(ml_dtypes.bfloat16)


def prep_inputs(cfg, x, w_in, idx_kn_g, idx_kn_b, conv_w, conv_b, dt_bias, a_log, d_skip, ssd_norm_g, w_out,
                ln1_g, ln1_b, w_rg, b_rg, w_re, b_re, w_gate, w_up, w_down, ln2_g, ln2_b):
    f = lambda a: np.ascontiguousarray(np.asarray(a, dtype=np.float32))
    x = f(x)
    w_in = f(w_in)[0]
    T, W, D = cfg.T, cfg.W, cfg.D
    bc = lambda v: np.ascontiguousarray(np.broadcast_to(f(v).reshape(1, -1), (128, f(v).size)))
    shared = {}
    shared["w_in"] = w_in
    ws = np.zeros((D, 128), np.float32)
    ws[:, 0:64] = w_in[:, cfg.o_ki:cfg.o_ki + 64]
    ws[:, 64:80] = w_in[:, cfg.o_wi:cfg.o_wi + 16]
    ws[:, 80:80 + cfg.SH] = w_in[:, cfg.o_dt:cfg.o_dt + cfg.SH]
    shared["w_small"] = ws
    shared["w_out"] = f(w_out)[0]
    shared["w_r"] = np.ascontiguousarray(np.concatenate([f(w_rg)[0], f(w_re)[0]], axis=1))
    shared["w_gate"] = f(w_gate)[0]
    shared["w_up"] = f(w_up)[0]
    shared["w_down"] = f(w_down)[0]
    shared["c_knb"] = np.ascontiguousarray(np.concatenate([bc(idx_kn_g), bc(idx_kn_b)], axis=1))
    shared["c_dtb"] = np.ascontiguousarray(np.concatenate([bc(dt_bias), bc(a_log)], axis=1))
    shared["c_dvec"] = bc(np.repeat(f(d_skip).reshape(-1), 64))
    shared["c_normg"] = bc(ssd_norm_g)
    NCC = cfg.XBC // 128
    cw = f(conv_w)[0]
    cbias = f(conv_b).reshape(-1)
    cc = np.zeros((128, NCC, 5), np.float32)
    cc[:, :, 0:4] = cw.T.reshape(NCC, 128, 4).transpose(1, 0, 2)
    cc[:, :, 4] = cbias.reshape(NCC, 128).T
    shared["c_conv"] = np.ascontiguousarray(cc.reshape(128, NCC * 5))
    shared["c_ln"] = np.ascontiguousarray(np.stack([f(ln1_g).reshape(-1), f(ln1_b).reshape(-1),
                                                    f(ln2_g).reshape(-1), f(ln2_b).reshape(-1)]))
    shared["c_rb"] = np.ascontiguousarray(np.concatenate([bc(b_rg), bc(b_re)], axis=1))
    cf, cb = _consts(cfg)
    shared["c_f32"] = cf
    shared["c_bf"] = cb
    in_maps = []
    for c in range(cfg.n_cores):
        b, h = c // 2, c % 2
        m = dict(shared)
        xw = np.zeros((W, D), np.float32)
        kp = np.arange(W, dtype=np.float32)
        if h == 1:
            xw[:] = x[b, 0:W]
        else:
            xw[T:] = x[b, 0:T]
            kp[:T] = 1e9
        m["xT_win"] = np.ascontiguousarray(xw.T)
        m["x_own"] = np.ascontiguousarray(xw[T:])
        m["kpos"] = np.ascontiguousarray(np.broadcast_to(kp[None, :], (128, W)))
        m["flag"] = np.full((128, 1), float(h), np.float32)
        in_maps.append(m)
    return in_maps


_NC_CACHE = {}


def run(cfg, inputs, debug=()):
    key = (id(cfg), tuple(debug))
    if key not in _NC_CACHE:
        _NC_CACHE[key] = Builder(cfg, debug).build()
    nc = _NC_CACHE[key]
    in_maps = prep_inputs(cfg, **inputs)
    res = run_bass_kernel_spmd(nc, in_maps, core_ids=list(range(cfg.n_cores)))
    return res.results


def kernel(**inputs):
    cfg = FULL
    results = run(cfg, inputs)
    out = np.zeros((cfg.B, cfg.SEQ, cfg.D), np.float32)
    for c in range(cfg.n_cores):
        b, h = c // 2, c % 2
        out[b, h * cfg.T:(h + 1) * cfg.T] = results[c]["out"]
    return out
```
